# Optimizing a Trainium2 kernel written in Bass

```python
import math
import jax, jax.numpy as jnp
from jax import lax
import numpy as np

D_MODEL = 1024
BATCH = 32
SEQ = 2048
DEPTH = 4

HEAD_DIM = 64
D_SSM = D_MODEL
SSM_HEADS = D_SSM // HEAD_DIM
SSM_GROUPS = 4
SSM_STATE = 128
CONV_WIDTH = 4
CHUNK = 128
D_ATT = D_MODEL
ATT_HEADS = D_ATT // HEAD_DIM
Q_BLOCK = 128
D_MIX = D_SSM + D_ATT
D_CONV = D_SSM + 2 * SSM_GROUPS * SSM_STATE
D_IN_PROJ = D_SSM + D_CONV + SSM_HEADS + 3 * D_ATT
D_FF = 4 * D_MODEL
EPS = 1e-5
DT_MIN = 1e-3
DT_MAX = 1e-1

kernel_name = "hybrid_ssd_stickbreaking_trunk"


def rmsnorm(x, g):
    xf = x.astype(jnp.float32)
    r = lax.rsqrt(jnp.mean(xf * xf, axis=-1, keepdims=True) + EPS)
    return (xf * r * g.astype(jnp.float32)).astype(x.dtype)


def causal_depthwise_conv(u, w, b):
    c = u.shape[-1]
    out = lax.conv_general_dilated(
        u, w[:, None, :].astype(u.dtype), window_strides=(1,),
        padding=[(CONV_WIDTH - 1, 0)],
        dimension_numbers=("NWC", "WIO", "NWC"), feature_group_count=c)
    return out + b


def ssd_chunked(x, dt, a_neg, bm, cm):
    b, L = x.shape[:2]
    nc = L // CHUNK
    r = SSM_HEADS // SSM_GROUPS
    xdt = (x * dt[..., None]).reshape(b, nc, CHUNK, SSM_GROUPS, r, HEAD_DIM)
    a = (dt * a_neg).reshape(b, nc, CHUNK, SSM_GROUPS, r)
    a = jnp.moveaxis(a, 2, -1)
    a_cum = jnp.cumsum(a, axis=-1)
    bc = bm.reshape(b, nc, CHUNK, SSM_GROUPS, SSM_STATE)
    cc = cm.reshape(b, nc, CHUNK, SSM_GROUPS, SSM_STATE)
    causal = jnp.tril(jnp.ones((CHUNK, CHUNK), dtype=bool))
    seg = a_cum[..., :, None] - a_cum[..., None, :]
    lmat = jnp.exp(jnp.where(causal, seg, -jnp.inf))
    cb = jnp.einsum("bclgn,bcsgn->bcgls", cc, bc)
    w = cb[:, :, :, None] * lmat
    y_diag = jnp.einsum("bcgrls,bcsgrp->bclgrp", w, xdt)
    decay_states = jnp.exp(a_cum[..., -1:] - a_cum)
    states = jnp.einsum("bclgn,bcgrl,bclgrp->bcgrpn", bc, decay_states, xdt)
    chunk_decay = jnp.exp(a_cum[..., -1])

    def step(h, inp):
        s_c, d_c = inp
        h_new = h * d_c[..., None, None] + s_c
        return h_new, h

    h0 = jnp.zeros_like(states[:, 0])
    _, prev_states = lax.scan(step, h0, (jnp.moveaxis(states, 1, 0), jnp.moveaxis(chunk_decay, 1, 0)))
    prev_states = jnp.moveaxis(prev_states, 0, 1)
    state_decay = jnp.moveaxis(jnp.exp(a_cum), -1, 2)[..., None]
    y_off = jnp.einsum("bclgn,bcgrpn->bclgrp", cc, prev_states) * state_decay
    return (y_diag + y_off).reshape(b, L, SSM_HEADS, HEAD_DIM)


def stick_breaking_attention(q, k, v):
    L = q.shape[1]
    scale = HEAD_DIM ** -0.5
    outs = []
    for i in range(L // Q_BLOCK):
        q0 = i * Q_BLOCK
        kv_len = q0 + Q_BLOCK
        qb = q[:, q0:kv_len]
        kb = k[:, :kv_len]
        vb = v[:, :kv_len]
        logits = jnp.einsum("bthd,bshd->bhts", qb, kb).astype(jnp.float32) * scale
        t_pos = q0 + jnp.arange(Q_BLOCK)
        s_pos = jnp.arange(kv_len)
        mask = s_pos[None, :] < t_pos[:, None]
        log_beta = jax.nn.log_sigmoid(logits)
        log_1m_beta = jnp.where(mask, jax.nn.log_sigmoid(-logits), 0.0)
        suffix = lax.cumsum(log_1m_beta, axis=log_1m_beta.ndim - 1, reverse=True) - log_1m_beta
        att = jnp.where(mask, jnp.exp(log_beta + suffix), 0.0)
        outs.append(jnp.einsum("bhts,bshd->bthd", att.astype(vb.dtype), vb))
    return jnp.concatenate(outs, axis=1)


def hybrid_layer(x, norm_mix_g, w_in, conv_w, conv_b, dt_bias, a_log, d_skip,
                 ssd_norm_g, att_norm_g, w_out, norm_mlp_g, w_up, w_down):
    b, L, _ = x.shape
    h = rmsnorm(x, norm_mix_g)
    proj = h @ w_in
    splits = [D_SSM, D_SSM + D_CONV, D_SSM + D_CONV + SSM_HEADS,
              D_SSM + D_CONV + SSM_HEADS + D_ATT,
              D_SSM + D_CONV + SSM_HEADS + 2 * D_ATT]
    z, xbc, dt_raw, q, k, v = jnp.split(proj, splits, axis=-1)
    xbc = jax.nn.silu(causal_depthwise_conv(xbc, conv_w, conv_b))
    xs, bm, cm = jnp.split(xbc, [D_SSM, D_SSM + SSM_GROUPS * SSM_STATE], axis=-1)
    dt = jax.nn.softplus(dt_raw.astype(jnp.float32) + dt_bias.astype(jnp.float32))
    a_neg = -jnp.exp(a_log.astype(jnp.float32))
    xs = xs.reshape(b, L, SSM_HEADS, HEAD_DIM)
    y = ssd_chunked(xs, dt, a_neg,
                    bm.reshape(b, L, SSM_GROUPS, SSM_STATE),
                    cm.reshape(b, L, SSM_GROUPS, SSM_STATE))
    y = (y + xs * d_skip[:, None]).reshape(b, L, D_SSM)
    y_ssd = rmsnorm(y * jax.nn.silu(z), ssd_norm_g)
    y_att = stick_breaking_attention(q.reshape(b, L, ATT_HEADS, HEAD_DIM),
                                     k.reshape(b, L, ATT_HEADS, HEAD_DIM),
                                     v.reshape(b, L, ATT_HEADS, HEAD_DIM))
    y_att = rmsnorm(y_att.reshape(b, L, D_ATT), att_norm_g)
    x = x + jnp.concatenate([y_ssd, y_att], axis=-1) @ w_out
    h = rmsnorm(x, norm_mlp_g)
    x = x + jnp.square(jax.nn.relu(h @ w_up)) @ w_down
    return x


def setup_inputs(seed: int = 0) -> dict:
    key = jax.random.key(seed)
    ks = jax.random.split(key, 16)
    f32 = jnp.float32
    nrm = lambda k, shape, s: jax.random.normal(k, shape, f32) * s
    x = jax.random.normal(ks[0], (BATCH, SEQ, D_MODEL), f32)
    norm_mix_g = 1.0 + nrm(ks[1], (DEPTH, D_MODEL), 0.02)
    w_in = nrm(ks[2], (DEPTH, D_MODEL, D_IN_PROJ), D_MODEL ** -0.5)
    conv_w = nrm(ks[3], (DEPTH, CONV_WIDTH, D_CONV), CONV_WIDTH ** -0.5)
    conv_b = nrm(ks[4], (DEPTH, D_CONV), 0.02)
    dt0 = jnp.exp(jax.random.uniform(ks[5], (DEPTH, SSM_HEADS), f32,
                                     math.log(DT_MIN), math.log(DT_MAX)))
    dt_bias = dt0 + jnp.log(-jnp.expm1(-dt0))
    a_log = jnp.log(jax.random.uniform(ks[6], (DEPTH, SSM_HEADS), f32, 1.0, 16.0))
    d_skip = 1.0 + nrm(ks[7], (DEPTH, SSM_HEADS), 0.02)
    ssd_norm_g = 1.0 + nrm(ks[8], (DEPTH, D_SSM), 0.02)
    att_norm_g = 1.0 + nrm(ks[9], (DEPTH, D_ATT), 0.02)
    w_out = nrm(ks[10], (DEPTH, D_MIX, D_MODEL), 0.5 * D_MIX ** -0.5)
    norm_mlp_g = 1.0 + nrm(ks[11], (DEPTH, D_MODEL), 0.02)
    w_up = nrm(ks[12], (DEPTH, D_MODEL, D_FF), D_MODEL ** -0.5)
    w_down = nrm(ks[13], (DEPTH, D_FF, D_MODEL), 0.5 * D_FF ** -0.5)
    final_norm_g = 1.0 + nrm(ks[14], (D_MODEL,), 0.02)
    return {"x": x, "norm_mix_g": norm_mix_g, "w_in": w_in, "conv_w": conv_w,
            "conv_b": conv_b, "dt_bias": dt_bias, "a_log": a_log, "d_skip": d_skip,
            "ssd_norm_g": ssd_norm_g, "att_norm_g": att_norm_g, "w_out": w_out,
            "norm_mlp_g": norm_mlp_g, "w_up": w_up, "w_down": w_down,
            "final_norm_g": final_norm_g}


def reference(x, norm_mix_g, w_in, conv_w, conv_b, dt_bias, a_log, d_skip,
              ssd_norm_g, att_norm_g, w_out, norm_mlp_g, w_up, w_down, final_norm_g):
    for l in range(DEPTH):
        x = hybrid_layer(x, norm_mix_g[l], w_in[l], conv_w[l], conv_b[l], dt_bias[l],
                         a_log[l], d_skip[l], ssd_norm_g[l], att_norm_g[l], w_out[l],
                         norm_mlp_g[l], w_up[l], w_down[l])
    return rmsnorm(x, final_norm_g)
```

```python
import os
import numpy as np
from contextlib import ExitStack
import concourse.bass as bass
import concourse.mybir as mybir
from concourse.bass_utils import run_bass_kernel_spmd

F32 = mybir.dt.float32
BF16 = mybir.dt.bfloat16
AF = mybir.ActivationFunctionType
ALU = mybir.AluOpType
AX = mybir.AxisListType

D = 1024
L = 2048
NBLK = 16
DEPTH = 4
NCORES = 8
BATCH = 32
DIN = 6160
EPS = 1e-5
OFF_Z, OFF_XBC, OFF_DT, OFF_Q, OFF_K, OFF_V = 0, 1024, 3072, 3088, 4112, 5136

T_Z, T_SSD, T_DT, T_QKV, T_WOUT, T_UP, T_DN = 0, 4, 8, 9, 17, 21, 29
NTILES = 37
TILE_ELEMS = 8 * 512
NRING = 2

ENGS = ("pe", "act", "dve", "pool", "sp")


class Buf:
    __slots__ = ("name", "last_w", "readers")

    def __init__(self, name=""):
        self.name = name
        self.last_w = None
        self.readers = []


def _merge(dst, src):
    for k, v in src.items():
        if dst.get(k, 0) < v:
            dst[k] = v


class Sched:
    def __init__(self):
        self.ops = {e: [] for e in ENGS}
        self.cnt = {e: 0 for e in ENGS}
        self.seen = {e: {} for e in ENGS}
        self.streams = {}
        self.nops = 0

    def _deps(self, reads, writes):
        need = {}
        for b in reads:
            if b.last_w is not None:
                _merge(need, b.last_w)
        for b in writes:
            if b.last_w is not None:
                _merge(need, b.last_w)
            for r in b.readers:
                _merge(need, r)
        return need

    def _waits(self, eng, need):
        seen = self.seen[eng]
        waits = []
        for k, v in need.items():
            if k == "pe" and eng == "pe":
                continue
            if seen.get(k, 0) < v:
                waits.append((k, v))
        _merge(seen, need)
        return waits

    def _commit(self, clock, reads, writes):
        for b in reads:
            b.readers.append(clock)
        for b in writes:
            b.last_w = clock
            b.readers = []

    def op(self, eng, fn, reads=(), writes=()):
        need = self._deps(reads, writes)
        waits = self._waits(eng, need)
        self.cnt[eng] += 1
        clock = dict(need)
        clock[eng] = self.cnt[eng]
        self.ops[eng].append((fn, waits, (eng, 1)))
        self._commit(clock, reads, writes)
        self.nops += 1

    def dma(self, eng, stream, fns, reads=(), writes=()):
        need = self._deps(reads, writes)
        waits = self._waits(eng, need)
        clock = dict(need)
        first = True
        for fn in fns:
            self.streams[stream] = self.streams.get(stream, 0) + 16
            self.ops[eng].append((fn, waits if first else [], ("dma:" + stream, 16)))
            first = False
            self.nops += 1
        clock["dma:" + stream] = self.streams[stream]
        self._commit(clock, reads, writes)

    def barrier(self):
        g = {}
        for e in ENGS:
            if self.cnt[e]:
                g[e] = self.cnt[e]
        for s, v in self.streams.items():
            g["dma:" + s] = v
        for e in ENGS:
            waits = self._waits(e, dict(g))
            if waits:
                self.ops[e].append((None, waits, None))

    def emit(self, nc, stack):
        sems = {}
        for e in ENGS:
            sems[e] = stack.enter_context(nc.semaphore("s_" + e))
        for s in self.streams:
            sems["dma:" + s] = stack.enter_context(nc.semaphore("d_" + s))
        ops = self.ops

        def replay(name, eng):
            for fn, waits, inc in ops[name]:
                for k, v in waits:
                    eng.wait_ge(sems[k], v)
                if fn is None:
                    continue
                ins = fn(eng)
                if inc is not None:
                    ins.then_inc(sems[inc[0]], inc[1])

        with nc.Block() as block:
            @block.tensor
            def _(eng):
                replay("pe", eng)

            @block.scalar
            def _(eng):
                replay("act", eng)

            @block.vector
            def _(eng):
                replay("dve", eng)

            @block.gpsimd
            def _(eng):
                replay("pool", eng)

            @block.sync
            def _(eng):
                replay("sp", eng)


def make_consts():
    p = np.arange(128)[:, None]
    f = np.arange(128)[None, :]
    ident = (p == f)
    tri_inc = (p <= f)
    tri_gt = (p > f)
    uneg = -(p >= f).astype(np.float32)
    mask_att = (p < f)
    c = np.concatenate([ident, tri_inc, tri_gt, uneg, mask_att], axis=1).astype(np.float32)
    return np.ascontiguousarray(c)


class Builder:
    def __init__(self, nseq, nl, stages=("att", "ssd", "mlp"), debug=False):
        self.nseq, self.nl, self.stages, self.debug = nseq, nl, stages, debug
        self.S = Sched()
        self.dumps = {}
        self.nc = nc = bass.Bass("TRN2", target_bir_lowering=False)
        dt = lambda name, shape, kind="ExternalInput", dty=F32: nc.dram_tensor(name, list(shape), dty, kind=kind).ap()
        self.x_in = dt("x", [nseq, L, D])
        self.norm_mix_g = dt("norm_mix_g", [nl, D])
        self.w_in = dt("w_in", [nl, D, DIN])
        self.conv_w = dt("conv_w", [nl, 4, 2048])
        self.conv_b = dt("conv_b", [nl, 2048])
        self.dt_bias = dt("dt_bias", [nl, 16])
        self.a_log = dt("a_log", [nl, 16])
        self.d_skip = dt("d_skip", [nl, 16])
        self.ssd_norm_g = dt("ssd_norm_g", [nl, D])
        self.att_norm_g = dt("att_norm_g", [nl, D])
        self.w_out = dt("w_out", [nl, 2048, D])
        self.norm_mlp_g = dt("norm_mlp_g", [nl, D])
        self.w_up = dt("w_up", [nl, D, 4096])
        self.w_down = dt("w_down", [nl, 4096, D])
        self.final_norm_g = dt("final_norm_g", [D])
        self.consts = dt("consts", [128, 5 * 128])
        self.out = dt("out", [nseq, L, D], kind="ExternalOutput")
        self.ws = nc.dram_tensor("ws", [nl, NTILES, 128, TILE_ELEMS], BF16).ap()

    def alloc(self, nbytes):
        off = self.aoff
        self.aoff += (nbytes + 63) // 64 * 64
        assert self.aoff <= self.acap, (self.aoff, self.acap)
        return off

    def view(self, shape, dtype):
        n = int(np.prod(shape))
        nb = n * (4 if dtype == F32 else 2)
        off = self.alloc(nb)
        ap = self.arena[:, off // 2: off // 2 + nb // 2]
        if dtype == F32:
            ap = ap.bitcast(F32)
        if len(shape) == 2:
            ap = ap.rearrange("p (a b) -> p a b", b=shape[1])
        elif len(shape) == 3:
            ap = ap.rearrange("p (a b c) -> p a b c", b=shape[1], c=shape[2])
        return ap

    def mm(self, out, lhsT, rhs, start, stop, r, w, skip=False):
        if skip:
            fn = lambda e: e.matmul(out, lhsT=lhsT, rhs=rhs, start=start, stop=stop, skip_group_check=True)
        else:
            fn = lambda e: e.matmul(out, lhsT=lhsT, rhs=rhs, start=start, stop=stop)
        self.S.op("pe", fn, reads=r, writes=w)

    def tp(self, out, in_, r, w):
        ident = self.ident_bf
        self.S.op("pe", lambda e: e.transpose(out, in_, ident), reads=r, writes=w)

    def act(self, out, in_, func, r, w, bias=None, scale=None, accum=None):
        kw = {}
        if bias is not None:
            kw["bias"] = bias
        if scale is not None:
            kw["scale"] = scale
        if accum is not None:
            kw["accum_out"] = accum
        self.S.op("act", lambda e: e.activation(out=out, in_=in_, func=func, **kw), reads=r, writes=w)

    def tt(self, eng, out, in0, in1, op, r, w):
        self.S.op(eng, lambda e: e.tensor_tensor(out=out, in0=in0, in1=in1, op=op), reads=r, writes=w)

    def ts(self, eng, out, in0, s1, op0, r, w, s2=None, op1=None):
        if op1 is None:
            fn = lambda e: e.tensor_scalar(out=out, in0=in0, scalar1=s1, scalar2=None, op0=op0)
        else:
            fn = lambda e: e.tensor_scalar(out=out, in0=in0, scalar1=s1, scalar2=s2, op0=op0, op1=op1)
        self.S.op(eng, fn, reads=r, writes=w)

    def stt(self, eng, out, in0, scalar, in1, op0, op1, r, w):
        self.S.op(eng, lambda e: e.scalar_tensor_tensor(out=out, in0=in0, scalar=scalar, in1=in1, op0=op0, op1=op1),
                  reads=r, writes=w)

    def cp(self, eng, out, in_, r, w):
        if eng == "act":
            self.S.op("act", lambda e: e.copy(out=out, in_=in_), reads=r, writes=w)
        else:
            self.S.op(eng, lambda e: e.tensor_copy(out=out, in_=in_), reads=r, writes=w)

    def memset(self, eng, ap, val, w):
        self.S.op(eng, lambda e: e.memset(ap, val), writes=w)

    def dump(self, name, ap, bufs):
        if not self.debug:
            return
        shape = list(ap.shape)
        d = self.nc.dram_tensor("dbg_" + name, shape, ap.dtype, kind="ExternalOutput").ap()
        self.dumps[name] = "dbg_" + name
        self.S.dma("sp", "dbg", [lambda e: e.dma_start(out=d, in_=ap)], reads=bufs, writes=[self.Bdbg])

    def prep_tile(self, l, t, pieces):
        fns = []
        for src, c0, w, wt in pieces:
            dst = self.ws[l, t][:, 0:8 * wt].rearrange("p (c n) -> p c n", n=wt)[:, :, c0:c0 + w]
            s = src.rearrange("(c p) n -> p c n", p=128)
            fns.append(lambda e, dst=dst, s=s: e.dma_start(out=dst, in_=s))
        self.S.dma("pool", "prep", fns, writes=[self.Bprep])

    def prep_weights(self):
        for l in range(self.nl):
            wi = self.w_in[l]
            for g in range(4):
                self.prep_tile(l, T_Z + g, [(wi[:, OFF_Z + g * 256: OFF_Z + (g + 1) * 256], 0, 256, 256)])
                xo = OFF_XBC
                self.prep_tile(l, T_SSD + g, [
                    (wi[:, xo + g * 256: xo + (g + 1) * 256], 0, 256, 512),
                    (wi[:, xo + 1024 + g * 128: xo + 1024 + (g + 1) * 128], 256, 128, 512),
                    (wi[:, xo + 1536 + g * 128: xo + 1536 + (g + 1) * 128], 384, 128, 512)])
            self.prep_tile(l, T_DT, [(wi[:, OFF_DT:OFF_DT + 16], 0, 16, 16)])
            for hp in range(8):
                self.prep_tile(l, T_QKV + hp, [
                    (wi[:, OFF_Q + hp * 128: OFF_Q + (hp + 1) * 128], 0, 128, 384),
                    (wi[:, OFF_K + hp * 128: OFF_K + (hp + 1) * 128], 128, 128, 384),
                    (wi[:, OFF_V + hp * 128: OFF_V + (hp + 1) * 128], 256, 128, 384)])
            for half in range(2):
                for nh in range(2):
                    self.prep_tile(l, T_WOUT + half * 2 + nh, [
                        (self.w_out[l][half * 1024:(half + 1) * 1024, nh * 512:(nh + 1) * 512], 0, 512, 512)])
            for i in range(8):
                self.prep_tile(l, T_UP + i, [(self.w_up[l][:, i * 512:(i + 1) * 512], 0, 512, 512)])
            for kg in range(4):
                for nh in range(2):
                    self.prep_tile(l, T_DN + kg * 2 + nh, [
                        (self.w_down[l][kg * 1024:(kg + 1) * 1024, nh * 512:(nh + 1) * 512], 0, 512, 512)])

    def load_w(self, l, t, w):
        i = self.ring_i
        self.ring_i = (i + 1) % len(self.ring)
        slot = self.ring[i]
        B = self.Bring[i]
        dst = slot[:, 0:8 * w]
        src = self.ws[l, t][:, 0:8 * w]
        self.S.dma("sp", "w%d" % i, [lambda e: e.dma_start(out=dst, in_=src)], reads=[self.Bprep], writes=[B])
        return dst.rearrange("p (c n) -> p c n", n=w), B

    def rstd_from_ss(self, ss, rstd, n, Bss, Brstd):
        self.act(self.lnv[:, 0:n], ss, AF.Ln, [Bss], [self.Blnv], bias=self.eps_col[:, 0:1], scale=1.0 / D)
        self.act(rstd, self.lnv[:, 0:n], AF.Exp, [self.Blnv], [Brstd], scale=-0.5)

    def norm_T(self, blocks, gcol, dst_fn, Bdst):
        n = len(blocks)
        self.memset("dve", self.ssx[:, 0:n], 0.0, [self.Bssx])
        for i, b in enumerate(blocks):
            self.act(self.junk[:, 0:D], self.x[:, b, :], AF.Square, [self.Bx[b]], [self.Bjunk, self.Bssx],
                     accum=self.ssx[:, i:i + 1])
        self.rstd_from_ss(self.ssx[:, 0:n], self.rstdx[:, 0:n], n, self.Bssx, self.Brstdx)
        for i, b in enumerate(blocks):
            k = i % 2
            self.ts("dve", self.hb[k], self.x[:, b, :], self.rstdx[:, i:i + 1], ALU.mult,
                    [self.Bx[b], self.Brstdx], [self.Bhb[k]])
            for c in range(8):
                self.tp(self.PT8[:, c, :], self.hb[k][:, c * 128:(c + 1) * 128], [self.Bhb[k]], self.BPT)
            self.tt("dve", dst_fn(i), self.PT8, gcol.unsqueeze(2).to_broadcast([128, 8, 128]), ALU.mult,
                    self.BPT + [self.Bpar], [Bdst])

    def att_phase(self, l):
        S = self.S
        m0 = self.aoff
        qT = self.view([L], BF16); kT = self.view([L], BF16); v = self.view([NBLK, 128], BF16)
        BqT, BkT, Bv = Buf("qT"), Buf("kT"), Buf("v")
        e_t = [self.view([512], F32) for _ in range(2)]
        sp = [self.view([512], BF16) for _ in range(3)]
        at = [self.view([512], BF16) for _ in range(3)]
        Be, Bsp, Bat = [Buf() for _ in range(2)], [Buf() for _ in range(3)], [Buf() for _ in range(3)]
        Tset = [[self.view([512], BF16) for _ in range(3)] for _ in range(2)]
        BTset = [[Buf() for _ in range(3)] for _ in range(2)]
        ycp = self.view([4, 128], BF16); Bycp = Buf("ycp")
        ssp = self.view([NBLK, 8], F32); Bssp = Buf("ssp")
        ssa = self.view([NBLK], F32); Bssa = Buf("ssa")
        rstda = self.view([NBLK], F32); Brstda = Buf("rstda")
        PA, PBk = self.P[2:4], self.P[4:6]
        BPA, BPBk = self.BP[2:4], self.BP[4:6]
        POs = [self.P[6].rearrange("p (a b) -> p a b", b=128), self.P[1].rearrange("p (a b) -> p a b", b=128)]
        BPOs = [self.BP[6], self.BP[1]]
        self.memset("dve", ssp, 0.0, [Bssp])
        gatt = self.gcols[:, l * 4 + 2, :]
        it = 0
        for hp in range(int(os.environ.get('K_NHP', 8))):
            wt, Bw = self.load_w(l, T_QKV + hp, 384)
            for tq in range(4):
                pb = tq % 2
                for c in range(8):
                    self.mm(self.P[pb], wt[:, c, 0:128], self.hT[:, c, tq * 512:(tq + 1) * 512], c == 0, c == 7,
                            [Bw, self.BhT], [self.BP[pb]])
                self.ts("dve", qT[:, tq * 512:(tq + 1) * 512], self.P[pb], 0.125, ALU.mult, [self.BP[pb]], [BqT])
            for tq in range(4):
                pb = tq % 2
                for c in range(8):
                    self.mm(self.P[pb], wt[:, c, 128:256], self.hT[:, c, tq * 512:(tq + 1) * 512], c == 0, c == 7,
                            [Bw, self.BhT], [self.BP[pb]])
                self.cp("dve", kT[:, tq * 512:(tq + 1) * 512], self.P[pb], [self.BP[pb]], [BkT])
            for b4 in range(int(os.environ.get('K_NV', 4))):
                pb = b4 % 2
                for j in range(4):
                    b = b4 * 4 + j
                    for c in range(8):
                        self.mm(self.P[pb][:, j * 128:(j + 1) * 128], self.hT[:, c, b * 128:(b + 1) * 128],
                                wt[:, c, 256:384], c == 0, c == 7, [Bw, self.BhT], [self.BP[pb]], skip=True)
                self.cp("dve", v[:, b4 * 4:(b4 + 1) * 4, :], self.P[pb].rearrange("p (a b) -> p a b", b=128),
                        [self.BP[pb]], [Bv])
            its = []
            gi = 0
            for c in range(4):
                for hh in range(2):
                    for kb in range(4 * c + 3, -1, -1):
                        its.append((c, hh, kb, gi))
                    gi += 1
            NI = len(its)

            def stage_ab(n):
                c, hh, kb, g = its[n]
                pr = slice(64 * hh, 64 * hh + 64)
                j0 = max(0, kb - 4 * c) * 128
                diag = kb >= 4 * c
                first = kb == 4 * c + 3
                T, BT_ = Tset[g % 2], BTset[g % 2]
                if first:
                    for q in range(3):
                        self.memset("pool", T[q], 0.0, [BT_[q]])
                i2, i3 = n % 2, n % 3
                kblk = kT[pr, kb * 128:(kb + 1) * 128]
                qsl = qT[pr, c * 512 + j0:(c + 1) * 512]
                self.mm(PA[i2][:, j0:512], kblk, qsl, True, True, [BkT, BqT], [BPA[i2]])
                self.act(e_t[i2][:, j0:512], PA[i2][:, j0:512], AF.Exp, [BPA[i2]], [Be[i2]])
                self.act(sp[i3][:, j0:512], e_t[i2][:, j0:512], AF.Ln, [Be[i2]], [Bsp[i3]], bias=self.one_col[:, 0:1])
                if diag:
                    self.tt("dve", sp[i3][:, j0:j0 + 128], sp[i3][:, j0:j0 + 128], self.maskatt_bf, ALU.mult,
                            [Bsp[i3]], [Bsp[i3]])
                if kb > 0:
                    ti, to = (n % 3), ((n + 1) % 3)
                    self.tt("pool", T[to][:, j0:512], T[ti][:, j0:512], sp[i3][:, j0:512], ALU.add,
                            [BT_[ti], Bsp[i3]], [BT_[to]])

            def stage_cd(n):
                c, hh, kb, g = its[n]
                pr = slice(64 * hh, 64 * hh + 64)
                j0 = max(0, kb - 4 * c) * 128
                diag = kb >= 4 * c
                first = kb == 4 * c + 3
                T, BT_ = Tset[g % 2], BTset[g % 2]
                i2, i3 = n % 2, n % 3
                kblk = kT[pr, kb * 128:(kb + 1) * 128]
                qsl = qT[pr, c * 512 + j0:(c + 1) * 512]
                self.mm(PBk[i2][:, j0:512], self.uneg_bf, sp[i3][:, j0:512], True, False, [Bsp[i3]], [BPBk[i2]])
                if not first:
                    self.mm(PBk[i2][:, j0:512], self.onesneg_bf, T[n % 3][:, j0:512], False, False, [BT_[n % 3]],
                            [BPBk[i2]])
                self.mm(PBk[i2][:, j0:512], kblk, qsl, False, True, [BkT, BqT], [BPBk[i2]])
                self.act(at[i3][:, j0:512], PBk[i2][:, j0:512], AF.Exp, [BPBk[i2]], [Bat[i3]])
                if diag:
                    self.tt("dve", at[i3][:, j0:j0 + 128], at[i3][:, j0:j0 + 128], self.maskatt_bf, ALU.mult,
                            [Bat[i3]], [Bat[i3]])

            def stage_e(n):
                c, hh, kb, g = its[n]
                j0 = max(0, kb - 4 * c) * 128
                first = kb == 4 * c + 3
                i3 = n % 3
                PO4, BPO = POs[c % 2], BPOs[c % 2]
                for qb in range(j0 // 128, 4):
                    self.mm(PO4[:, qb, 64 * hh:64 * hh + 64], at[i3][:, qb * 128:(qb + 1) * 128],
                            v[:, kb, 64 * hh:64 * hh + 64], first and qb == 3, kb == 0,
                            [Bat[i3], Bv], [BPO], skip=True)
                if hh == 1 and kb == 0:
                    self.cp("dve", ycp, PO4, [BPO], [Bycp])
                    for qb in range(4):
                        self.act(self.junk[:, 0:128], ycp[:, qb, :], AF.Square, [Bycp], [self.Bjunk, Bssp],
                                 accum=ssp[:, 4 * c + qb, hp:hp + 1])
                    k = c % 2
                    for qb in range(4):
                        self.tp(self.PT[k][:, qb, :], ycp[:, qb, :], [Bycp], [self.BPT[k]])
                    self.tt("dve", self.YT[:, hp, c * 512:(c + 1) * 512],
                            self.PT[k][:, 0:4, :].rearrange("p a b -> p (a b)"),
                            gatt[:, hp:hp + 1].to_broadcast([128, 512]), ALU.mult,
                            [self.BPT[k], self.Bpar], [self.BYT])

            for t in range(NI + 2):
                if t < NI:
                    stage_ab(t)
                if 0 <= t - 1 < NI:
                    stage_cd(t - 1)
                if 0 <= t - 2 < NI:
                    stage_e(t - 2)
            if self.debug and hp == 0:
                self.dump("qT", qT, [BqT]); self.dump("kT", kT, [BkT]); self.dump("v", v, [Bv])
        S.op("dve", lambda e: e.tensor_reduce(out=ssa, in_=ssp, axis=AX.X, op=ALU.add), reads=[Bssp], writes=[Bssa])
        self.rstd_from_ss(ssa, rstda, NBLK, Bssa, Brstda)
        self.dump("YT_att", self.YT, [self.BYT])
        self.dump("rstd_att", rstda, [Brstda])
        if not os.environ.get('K_NOOUT'):
            self.outproj(l, 1, rstda, Brstda)
        S.barrier()
        self.aoff = m0

    def outproj(self, l, half, rstd, Brstd):
        for nh in range(2):
            wt, Bw = self.load_w(l, T_WOUT + half * 2 + nh, 512)
            for b in range(NBLK):
                pb = b % 2
                for c in range(8):
                    self.mm(self.P[pb], self.YT[:, c, b * 128:(b + 1) * 128], wt[:, c, :], c == 0, c == 7,
                            [self.BYT, Bw], [self.BP[pb]])
                xs = self.x[:, b, nh * 512:(nh + 1) * 512]
                self.act(self.relu_t[pb], self.P[pb], AF.Copy, [self.BP[pb], Brstd], [self.Brelu[pb]],
                         scale=rstd[:, b:b + 1])
                self.tt("dve", xs, self.relu_t[pb], xs, ALU.add, [self.Brelu[pb], self.Bx[b]], [self.Bx[b]])

    def ssd_phase(self, l):
        S = self.S
        m0 = self.aoff
        P = self.P; BP = self.BP
        f256 = lambda: self.view([NBLK, 16], F32)
        dtv, av, acum, sdec, dsv, cdv = [f256() for _ in range(6)]
        dtr = acum
        Bdt = Buf("dtstuff")
        ssp = self.view([NBLK, 4], F32); Bssp = Buf("ssp")
        sss = self.view([NBLK], F32); Bsss = Buf("sss")
        rstds = self.view([NBLK], F32); Brstds = Buf("rstds")
        ust = [self.view([515], F32) for _ in range(2)]; Bust = [Buf(), Buf()]
        acc = self.view([512], F32); Bacc = Buf("acc")
        xsT = self.view([L], BF16) if False else None; BxsT = Buf("xsT")
        xst2 = [self.view([512], BF16) for _ in range(2)]; Bxst2 = [Buf(), Buf()]
        BT = self.view([L], BF16); BBT = Buf("BT")
        CT = self.view([L], BF16); BCT = Buf("CT")
        xs_tok = self.view([NBLK, 256], BF16); Bxst = Buf("xs_tok")
        B_tok = self.view([NBLK, 128], BF16); BBtok = Buf("B_tok")
        r4v = lambda t: t.rearrange("p (a b) -> p a b", b=64)
        cbm = [self.view([128], F32) for _ in range(2)]; Bcbm = [Buf(), Buf()]
        Am = [self.hb[k].bitcast(F32).rearrange("p (a b) -> p a b", b=128) for k in range(2)]; BAm = self.Bhb
        E = [self.view([4, 128], BF16) for _ in range(2)]; BE = [Buf(), Buf()]
        WT = [self.view([4, 128], BF16) for _ in range(2)]; BWT = [Buf(), Buf()]
        xdt = [self.view([4, 64], BF16) for _ in range(2)]; Bxdt = [Buf(), Buf()]
        xdd = [self.view([4, 64], BF16) for _ in range(2)]; Bxdd = [Buf(), Buf()]
        xsd = [r4v(self.relu_t[k][:, 0:256]) for k in range(2)]; Bxsd = [Buf(), Buf()]
        zs = [r4v(self.relu_t[k][:, 256:512]) for k in range(2)]; Bzs = [Buf(), Buf()]
        t1 = self.view([4, 64], F32); Bt1 = Buf("t1")
        t2 = self.view([4, 64], F32); Bt2 = Buf("t2")
        yg = self.view([256], BF16); Byg = Buf("yg")
        prev_f = self.view([4, 64], F32); Bpf = Buf("prev_f")
        ptmp = self.view([4, 64], F32); Bptmp = Buf("ptmp")
        prev_b = [self.view([256], BF16) for _ in range(2)]; Bpb = [Buf(), Buf()]
        wdt, Bwdt = self.load_w(l, T_DT, 16)
        Pdt = P[2][:, 0:256].rearrange("p (a b) -> p a b", b=16)
        for b in range(NBLK):
            for c in range(8):
                self.mm(Pdt[:, b, :], self.hT[:, c, b * 128:(b + 1) * 128], wdt[:, c, :], c == 0, c == 7,
                        [self.BhT, Bwdt], [BP[2]], skip=True)
        bc16 = lambda t: t.unsqueeze(1).to_broadcast([128, NBLK, 16])
        self.tt("dve", dtr, Pdt, bc16(self.dtb[:, l, :]), ALU.add, [BP[2], self.Bpar], [Bdt])
        self.act(dtr, dtr, AF.Exp, [Bdt], [Bdt])
        self.act(dtv, dtr, AF.Ln, [Bdt], [Bdt], bias=self.one_col[:, 0:1])
        self.tt("dve", av, dtv, bc16(self.aneg[:, l, :]), ALU.mult, [Bdt, self.Bpar], [Bdt])
        av2 = av.rearrange("p a b -> p (a b)")
        self.mm(P[3][:, 0:256], self.tri_inc, av2, True, True, [Bdt], [BP[3]])
        self.mm(P[3][:, 256:512], self.ones_f, av2, True, True, [Bdt], [BP[3]], skip=True)
        fl = lambda t: t.rearrange("p a b -> p (a b)")
        self.act(fl(sdec), P[3][:, 0:256], AF.Exp, [BP[3]], [Bdt])
        self.cp("dve", fl(acum), P[3][:, 0:256], [BP[3]], [Bdt])
        self.tt("dve", fl(dsv), P[3][:, 256:512], fl(acum), ALU.subtract, [BP[3], Bdt], [Bdt])
        self.act(fl(dsv), fl(dsv), AF.Exp, [Bdt], [Bdt])
        self.act(fl(cdv), P[3][:, 256:512], AF.Exp, [BP[3]], [Bdt])
        self.dump("dt", dtv, [Bdt]); self.dump("acum", acum, [Bdt]); self.dump("cd", cdv, [Bdt])
        self.memset("dve", ssp, 0.0, [Bssp])
        gssd = self.gcols[:, l * 4 + 1, :]

        for g in range(4):
            wz, Bwz = self.load_w(l, T_Z + g, 256)
            wsd, Bwsd = self.load_w(l, T_SSD + g, 512)
            chids = [2 * g, 2 * g + 1, 8 + g, 12 + g]
            ui = 0
            for ci in range(4):
                ch = chids[ci]
                for tq in range(4):
                    pb = tq % 2
                    for c in range(8):
                        self.mm(P[pb], wsd[:, c, ci * 128:(ci + 1) * 128], self.hT[:, c, tq * 512:(tq + 1) * 512],
                                c == 0, c == 7, [Bwsd, self.BhT], [BP[pb]])
                    k = ui % 2
                    ui += 1
                    if tq == 0:
                        self.memset("dve", ust[k][:, 0:3], 0.0, [Bust[k]])
                    else:
                        self.cp("dve", ust[k][:, 0:3], ust[1 - k][:, 512:515], [Bust[1 - k]], [Bust[k]])
                    self.cp("act", ust[k][:, 3:515], P[pb], [BP[pb]], [Bust[k]])
                    cw = lambda j: self.cw[:, l, ch, j:j + 1]
                    self.ts("dve", acc, ust[k][:, 3:515], cw(3), ALU.mult, [Bust[k], self.Bpar], [Bacc],
                            s2=self.cb[:, l, ch:ch + 1], op1=ALU.add)
                    for j in (2, 1, 0):
                        self.stt("dve", acc, ust[k][:, j:j + 512], cw(j), acc, ALU.mult, ALU.add,
                                 [Bust[k], self.Bpar, Bacc], [Bacc])
                    if ci < 2:
                        src, Bsrc = xst2[k], Bxst2[k]
                        self.act(src, acc, AF.Silu, [Bacc], [Bsrc])
                    elif ci == 2:
                        src, Bsrc = BT[:, tq * 512:(tq + 1) * 512], BBT
                        self.act(src, acc, AF.Silu, [Bacc], [Bsrc])
                    else:
                        self.act(CT[:, tq * 512:(tq + 1) * 512], acc, AF.Silu, [Bacc], [BCT])
                    if ci < 3:
                        for j in range(4):
                            self.tp(self.PT[k][:, j, :], src[:, j * 128:(j + 1) * 128], [Bsrc], [self.BPT[k]])
                        if ci < 2:
                            self.cp("dve", xs_tok[:, tq * 4:(tq + 1) * 4, ci * 128:(ci + 1) * 128], self.PT[k],
                                    [self.BPT[k]], [Bxst])
                        else:
                            self.cp("dve", B_tok[:, tq * 4:(tq + 1) * 4, :], self.PT[k], [self.BPT[k]], [BBtok])
            if g == 0:
                self.dump("xs_tok", xs_tok, [Bxst]); self.dump("BT", BT, [BBT]); self.dump("CT", CT, [BCT])
            self.memset("dve", prev_f, 0.0, [Bpf])
            self.memset("dve", prev_b[0], 0.0, [Bpb[0]])
            hs = slice(4 * g, 4 * g + 4)
            bcl = lambda t, n: t.unsqueeze(2).to_broadcast([128, 4, n])
            r4 = lambda t: t.rearrange("p (a b) -> p a b", b=64)
            sets = [(2, 3, 4), (5, 6, 0)]

            def s1(b):
                p = b % 2
                iA, iB, iC = sets[p]
                blk = slice(b * 128, (b + 1) * 128)
                Pcb = P[iA][:, 0:128]; Pyd = P[iA][:, 128:384]; Pseg = P[iB]; Pst = P[iC][:, 0:256]; Pz = P[iC][:, 256:512]
                xsb = xs_tok[:, b, :].rearrange("p (a b) -> p a b", b=64)
                self.tt("pool", xdt[p], xsb, bcl(dtv[:, b, hs], 64), ALU.mult, [Bxst, Bdt], [Bxdt[p]])
                self.tt("pool", xdd[p], xdt[p], bcl(dsv[:, b, hs], 64), ALU.mult, [Bxdt[p], Bdt], [Bxdd[p]])
                self.tt("pool", xsd[p], xsb, bcl(self.dsk[:, l, hs], 64), ALU.mult, [Bxst, self.Bpar], [Bxsd[p]])
                self.tt("dve", Am[p], self.tri_gt.unsqueeze(1).to_broadcast([128, 4, 128]), bcl(av[:, b, hs], 128),
                        ALU.mult, [Bdt], [BAm[p]])
                self.mm(Pcb, BT[:, blk], CT[:, blk], True, True, [BBT, BCT], [BP[iA]])
                for c in range(8):
                    self.mm(Pz, self.hT[:, c, blk], wz[:, c, :], c == 0, c == 7, [self.BhT, Bwz], [BP[iC]], skip=True)
                self.mm(Pst, B_tok[:, b, :], fl(xdd[p]), True, True, [BBtok, Bxdd[p]], [BP[iC]], skip=True)
                for h in range(4):
                    self.mm(Pseg[:, h * 128:(h + 1) * 128], Am[p][:, h, :], self.tri_inc, True, True, [BAm[p]], [BP[iB]],
                            skip=True)
                self.tt("dve", cbm[p], Pcb, self.tri_inc, ALU.mult, [BP[iA]], [Bcbm[p]])
                self.act(fl(E[p]), Pseg, AF.Exp, [BP[iB]], [BE[p]])
                self.act(fl(zs[p]), Pz, AF.Silu, [BP[iC]], [Bzs[p]])
                self.tt("dve", WT[p], E[p], cbm[p].unsqueeze(1).to_broadcast([128, 4, 128]), ALU.mult,
                        [BE[p], Bcbm[p]], [BWT[p]])
                for h in range(4):
                    self.mm(Pyd[:, h * 64:(h + 1) * 64], WT[p][:, h, :], xdt[p][:, h, :], True, True,
                            [BWT[p], Bxdt[p]], [BP[iA]], skip=True)

            def s2(b):
                p = b % 2
                iA, iB, iC = sets[p]
                blk = slice(b * 128, (b + 1) * 128)
                Pyd = P[iA][:, 128:384]; Pst = P[iC][:, 0:256]
                Pyo = P[1][:, 0:256]
                self.mm(Pyo, CT[:, blk], prev_b[p], True, True, [BCT, Bpb[p]], [BP[1]], skip=True)
                self.tt("dve", ptmp, prev_f, bcl(cdv[:, b, hs], 64), ALU.mult, [Bpf, Bdt], [Bptmp])
                self.tt("dve", prev_f, ptmp, r4(Pst), ALU.add, [Bptmp, BP[iC]], [Bpf])
                self.cp("act", prev_b[1 - p], fl(prev_f), [Bpf], [Bpb[1 - p]])
                self.tt("dve", t1, r4(Pyo), bcl(sdec[:, b, hs], 64), ALU.mult, [BP[1], Bdt], [Bt1])
                self.tt("dve", t2, r4(Pyd), t1, ALU.add, [BP[iA], Bt1], [Bt2])
                self.tt("dve", t2, t2, xsd[p], ALU.add, [Bt2, Bxsd[p]], [Bt2])
                self.tt("dve", r4(yg), t2, zs[p], ALU.mult, [Bt2, Bzs[p]], [Byg])
                self.act(self.junk[:, 0:256], yg, AF.Square, [Byg], [self.Bjunk, Bssp], accum=ssp[:, b, g:g + 1])
                k = b % 2
                for j in range(2):
                    self.tp(self.PT[k][:, j, :], yg[:, j * 128:(j + 1) * 128], [Byg], [self.BPT[k]])
                self.tt("dve", self.YT[:, 2 * g:2 * g + 2, blk], self.PT[k][:, 0:2, :],
                        gssd[:, 2 * g:2 * g + 2].unsqueeze(2).to_broadcast([128, 2, 128]), ALU.mult,
                        [self.BPT[k], self.Bpar], [self.BYT])

            s1(0)
            for b in range(NBLK):
                if b + 1 < NBLK:
                    s1(b + 1)
                s2(b)
        S.barrier()
        S.op("dve", lambda e: e.tensor_reduce(out=sss, in_=ssp, axis=AX.X, op=ALU.add), reads=[Bssp], writes=[Bsss])
        self.rstd_from_ss(sss, rstds, NBLK, Bsss, Brstds)
        self.dump("YT_ssd", self.YT, [self.BYT])
        self.dump("rstd_ssd", rstds, [Brstds])
        self.outproj(l, 0, rstds, Brstds)
        S.barrier()
        self.aoff = m0

    def mlp_phase(self, l):
        S = self.S
        m0 = self.aoff
        P = self.P; BP = self.BP
        h2T = self.view([8, 512], BF16); Bh2 = Buf("h2T")
        uT = self.YT.rearrange("p a b -> p (a b)").rearrange("p (a b) -> p a b", b=512)
        BuT = self.BYT
        gm = self.gcols[:, l * 4 + 3, :]
        for tc in range(4):
            blocks = list(range(4 * tc, 4 * tc + 4))
            self.norm_T(blocks, gm, lambda i: h2T[:, :, i * 128:(i + 1) * 128], Bh2)
            for nb4 in range(8):
                wt, Bw = self.load_w(l, T_UP + nb4, 512)
                for j in range(4):
                    nb = nb4 * 4 + j
                    pb = nb % 2
                    for c in range(8):
                        self.mm(P[pb], wt[:, c, j * 128:(j + 1) * 128], h2T[:, c, :], c == 0, c == 7, [Bw, Bh2], [BP[pb]])
                    self.act(self.relu_t[pb], P[pb], AF.Relu, [BP[pb]], [self.Brelu[pb]])
                    self.tt("dve", uT[:, nb, :], self.relu_t[pb], self.relu_t[pb], ALU.mult, [self.Brelu[pb]], [BuT])
            for nh in range(2):
                for kg in range(4):
                    wt, Bw = self.load_w(l, T_DN + kg * 2 + nh, 512)
                    for b in range(4):
                        for kk in range(8):
                            k = kg * 8 + kk
                            self.mm(P[2 + b], uT[:, k, b * 128:(b + 1) * 128], wt[:, kk, :], k == 0, k == 31,
                                    [BuT, Bw], [BP[2 + b]])
                for b in range(4):
                    xb = 4 * tc + b
                    xs = self.x[:, xb, nh * 512:(nh + 1) * 512]
                    self.tt("dve", xs, P[2 + b], xs, ALU.add, [BP[2 + b], self.Bx[xb]], [self.Bx[xb]])
        S.barrier()
        self.aoff = m0

    def build(self):
        nc, S = self.nc, self.S
        nl = self.nl
        with ExitStack() as st:
            self.acap = 207 * 1024
            self.arena = st.enter_context(nc.sbuf_tensor("arena", [128, self.acap // 2], BF16))
            self.aoff = 0
            self.P = [st.enter_context(nc.psum_tensor("ps%d" % i, [128, 512], F32)) for i in range(7)]
            self.P = [p[:, :] for p in self.P]
            self.BP = [Buf("P%d" % i) for i in range(7)]
            ptt = st.enter_context(nc.psum_tensor("pst", [128, 1024], BF16))
            self.PT = [ptt[:, 0:512].rearrange("p (a b) -> p a b", b=128),
                       ptt[:, 512:1024].rearrange("p (a b) -> p a b", b=128)]
            self.PT8 = ptt[:, :].rearrange("p (a b) -> p a b", b=128)
            self.BPT = [Buf("PT0"), Buf("PT1")]
            self.Bdbg = Buf("dbg"); self.Bprep = Buf("prep"); self.Bpar = Buf("par")

            self.x = self.view([NBLK, D], F32); self.Bx = [Buf("x%d" % b) for b in range(NBLK)]
            self.hT = self.view([8, L], BF16); self.BhT = Buf("hT")
            self.YT = self.view([8, L], BF16); self.BYT = Buf("YT")
            self.ring = [self.view([TILE_ELEMS], BF16) for _ in range(NRING)]
            self.Bring = [Buf("ring%d" % i) for i in range(NRING)]
            self.ring_i = 0
            cst = self.YT.rearrange("p a b -> p (a b)")[:, 0:1280].bitcast(F32)
            tri2 = self.view([256], F32)
            self.ident_bf = self.view([128], BF16); self.uneg_bf = self.view([128], BF16)
            self.onesneg_bf = self.view([128], BF16); self.maskatt_bf = self.view([128], BF16)
            self.tri_inc = tri2[:, 0:128]; self.tri_gt = tri2[:, 128:256]
            self.ones_f = self.view([128], F32)
            self.gcols = self.view([nl * 4 + 1, 8], F32)
            self.cw = self.view([nl, 16, 4], F32); self.cb = self.view([nl, 16], F32)
            self.dtb = self.view([nl, 16], F32); self.aneg = self.view([nl, 16], F32); self.dsk = self.view([nl, 16], F32)
            self.eps_col = self.view([1], F32); self.one_col = self.view([1], F32)
            self.ssx = self.view([NBLK], F32); self.Bssx = Buf("ssx")
            self.rstdx = self.view([NBLK], F32); self.Brstdx = Buf("rstdx")
            self.lnv = self.view([NBLK], F32); self.Blnv = Buf("lnv")
            self.junk = self.view([D], BF16); self.Bjunk = Buf("junk")
            self.hb = [self.view([D], BF16) for _ in range(2)]; self.Bhb = [Buf(), Buf()]
            self.relu_t = [self.view([512], F32) for _ in range(2)]; self.Brelu = [Buf(), Buf()]
            hTf = self.hT.rearrange("p a b -> p (a b)")
            self.ost = [hTf[:, k * 2048:(k + 1) * 2048].bitcast(F32) for k in range(2)]; self.Bost = [Buf(), Buf()]
            self.gfin = hTf[:, 4096:6144].bitcast(F32)

            Bpar = self.Bpar
            S.dma("sp", "par", [lambda e: e.dma_start(out=cst, in_=self.consts)], writes=[Bpar])
            gl = []
            for l in range(nl):
                for j, gsrc in enumerate((self.norm_mix_g, self.ssd_norm_g, self.att_norm_g, self.norm_mlp_g)):
                    gl.append((self.gcols[:, l * 4 + j, :], gsrc[l]))
            gl.append((self.gcols[:, nl * 4, :], self.final_norm_g))
            fns = [(lambda e, d=d, s=s: e.dma_start(out=d, in_=s.rearrange("(c p) -> p c", p=128),
                                                    allow_slow_non_contiguous=True)) for d, s in gl]
            for l in range(nl):
                for kk in range(4):
                    fns.append(lambda e, l=l, kk=kk: e.dma_start(out=self.cw[:, l, :, kk],
                                                             in_=self.conv_w[l, kk].rearrange("(c p) -> p c", p=128),
                                                             allow_slow_non_contiguous=True))
                fns.append(lambda e, l=l: e.dma_start(out=self.cb[:, l, :],
                                                      in_=self.conv_b[l].rearrange("(c p) -> p c", p=128),
                                                      allow_slow_non_contiguous=True))
            fl2 = lambda t: t.rearrange("p a b -> p (a b)")
            fns.append(lambda e: e.dma_start(out=fl2(self.dtb), in_=self.dt_bias.rearrange("a b -> (a b)").partition_broadcast(128)))
            fns.append(lambda e: e.dma_start(out=fl2(self.aneg), in_=self.a_log.rearrange("a b -> (a b)").partition_broadcast(128)))
            fns.append(lambda e: e.dma_start(out=fl2(self.dsk), in_=self.d_skip.rearrange("a b -> (a b)").partition_broadcast(128)))
            S.dma("sp", "par", fns, writes=[Bpar])
            self.cp("dve", self.ident_bf, cst[:, 0:128], [Bpar], [Bpar])
            self.cp("dve", tri2, cst[:, 128:384], [Bpar], [Bpar])
            self.cp("dve", self.uneg_bf, cst[:, 384:512], [Bpar], [Bpar])
            self.cp("dve", self.maskatt_bf, cst[:, 512:640], [Bpar], [Bpar])
            self.memset("dve", self.onesneg_bf, -1.0, [Bpar])
            self.memset("dve", self.ones_f, 1.0, [Bpar])
            self.memset("dve", self.eps_col, EPS, [Bpar])
            self.memset("dve", self.one_col, 1.0, [Bpar])
            self.act(fl2(self.aneg), fl2(self.aneg), AF.Exp, [Bpar], [Bpar])
            self.ts("dve", fl2(self.aneg), fl2(self.aneg), -1.0, ALU.mult, [Bpar], [Bpar])
            self.prep_weights()
            S.barrier()

            for s in range(self.nseq):
                S.dma("sp", "x", [lambda e, s=s: e.dma_start(out=self.x, in_=self.x_in[s].rearrange("(b p) d -> p b d", p=128))],
                      writes=self.Bx)
                for l in range(nl):
                    self.norm_T(list(range(NBLK)), self.gcols[:, l * 4 + 0, :],
                                lambda i: self.hT[:, :, i * 128:(i + 1) * 128], self.BhT)
                    if self.debug and s == 0 and l == 0:
                        self.dump("hT", self.hT, [self.BhT])
                    S.barrier()
                    if "att" in self.stages:
                        self.att_phase(l)
                    if "ssd" in self.stages:
                        self.ssd_phase(l)
                    if self.debug and s == 0 and l == 0:
                        self.dump("x_mix", self.x, self.Bx)
                    if "mlp" in self.stages:
                        self.mlp_phase(l)
                S.barrier()
                S.dma("sp", "par", [lambda e: e.dma_start(out=self.gfin, in_=self.final_norm_g.partition_broadcast(128))],
                      writes=[Bpar])
                self.memset("dve", self.ssx, 0.0, [self.Bssx])
                for b in range(NBLK):
                    self.act(self.junk[:, 0:D], self.x[:, b, :], AF.Square, [self.Bx[b]], [self.Bjunk, self.Bssx],
                             accum=self.ssx[:, b:b + 1])
                self.rstd_from_ss(self.ssx, self.rstdx, NBLK, self.Bssx, self.Brstdx)
                for b in range(NBLK):
                    k = b % 2
                    self.stt("dve", self.ost[k], self.x[:, b, :], self.rstdx[:, b:b + 1], self.gfin, ALU.mult, ALU.mult,
                             [self.Bx[b], self.Brstdx, Bpar], [self.Bost[k]])
                    S.dma("sp", "o%d" % k,
                          [lambda e, s=s, b=b, k=k: e.dma_start(out=self.out[s, b * 128:(b + 1) * 128, :], in_=self.ost[k])],
                          reads=[self.Bost[k]], writes=[])
            S.barrier()
            S.emit(nc, st)
        return nc


_CACHE = {}


def _get_nc(nseq, nl):
    key = (nseq, nl)
    if key not in _CACHE:
        _CACHE[key] = Builder(nseq, nl).build()
    return _CACHE[key]


def kernel(x, norm_mix_g, w_in, conv_w, conv_b, dt_bias, a_log, d_skip, ssd_norm_g, att_norm_g, w_out,
           norm_mlp_g, w_up, w_down, final_norm_g):
    f = lambda a: np.ascontiguousarray(np.asarray(a, dtype=np.float32))
    x = f(x)
    nseq = BATCH // NCORES
    nc = _get_nc(nseq, DEPTH)
    shared = dict(norm_mix_g=f(norm_mix_g), w_in=f(w_in), conv_w=f(conv_w), conv_b=f(conv_b), dt_bias=f(dt_bias),
                  a_log=f(a_log), d_skip=f(d_skip), ssd_norm_g=f(ssd_norm_g), att_norm_g=f(att_norm_g),
                  w_out=f(w_out), norm_mlp_g=f(norm_mlp_g), w_up=f(w_up), w_down=f(w_down),
                  final_norm_g=f(final_norm_g), consts=make_consts())
    in_maps = []
    for c in range(NCORES):
        m = dict(shared)
        m["x"] = np.ascontiguousarray(x[c * nseq:(c + 1) * nseq])
        in_maps.append(m)
    res = run_bass_kernel_spmd(nc, in_maps, core_ids=list(range(NCORES)))
    return np.concatenate([np.asarray(r["out"]) for r in res.results], axis=0).astype(np.float32)
```
